# Optimizing a Trainium2 kernel written in Bass

```python
import jax, jax.numpy as jnp
from jax import lax
import numpy as np

D_MODEL = 1024
BATCH = 4
SEQ = 8192
DEPTH = 4

N_MIXERS = 3
EXPAND = 2
D_INNER = EXPAND * D_MODEL
CHUNK = 128
A_GROUPS = 8
A_GROUP_DIM = D_INNER // A_GROUPS
POOL_WINDOWS = (2, 4, 8, 16)
B_GROUPS = len(POOL_WINDOWS)
B_GROUP_DIM = D_INNER // B_GROUPS
CONV_WIDTH = 3
LN_EPS = 1e-5
ALPHA = (2.0 * DEPTH) ** 0.25
BETA = (8.0 * DEPTH) ** -0.25

kernel_name = "hybrid_gmlp_pool_shortconv_deepnorm"


def layer_norm(x, g, b):
    xf = x.astype(jnp.float32)
    mu = jnp.mean(xf, axis=-1, keepdims=True)
    var = jnp.mean(jnp.square(xf - mu), axis=-1, keepdims=True)
    y = (xf - mu) * lax.rsqrt(var + LN_EPS)
    return (y * g.astype(jnp.float32) + b.astype(jnp.float32)).astype(x.dtype)


def mixer_a(h, w_in, v_gain, v_bias, w_s, b_s, w_out):
    bsz, s, _ = h.shape
    u, v, z = jnp.split(h @ w_in, 3, axis=-1)
    u = jax.nn.gelu(u)
    v = layer_norm(jax.nn.gelu(v), v_gain, v_bias)
    v = v.reshape(bsz, s // CHUNK, CHUNK, A_GROUPS, A_GROUP_DIM)
    w_causal = w_s * jnp.tril(jnp.ones((CHUNK, CHUNK), w_s.dtype))
    sv = jnp.einsum('gts,bcsgd->bctgd', w_causal, v) + b_s.T[:, :, None]
    sv = sv.reshape(bsz, s, D_INNER)
    return (u * sv * jax.nn.silu(z)) @ w_out


def trailing_mean_minus_self(v, window):
    s = v.shape[1]
    vf = v.astype(jnp.float32)
    csum = jnp.cumsum(vf, axis=1)
    lag = jnp.pad(csum, ((0, 0), (window, 0), (0, 0)))[:, :s]
    count = jnp.minimum(jnp.arange(1, s + 1), window).astype(jnp.float32)[None, :, None]
    return ((csum - lag) / count - vf).astype(v.dtype)


def mixer_b(h, w_in, w_grp, scale, w_out):
    bsz, s, _ = h.shape
    v, z = jnp.split(h @ w_in, 2, axis=-1)
    groups = jnp.split(v, B_GROUPS, axis=-1)
    pooled = jnp.stack([trailing_mean_minus_self(g, w) for g, w in zip(groups, POOL_WINDOWS)], axis=2)
    mixed = jnp.einsum('bsgd,gde->bsge', pooled, w_grp).reshape(bsz, s, D_INNER)
    return (mixed * scale * jax.nn.silu(z)) @ w_out


def causal_depthwise_conv(x, w):
    k = w.shape[0]
    s = x.shape[1]
    xp = jnp.pad(x, ((0, 0), (k - 1, 0), (0, 0)))
    out = xp[:, 0:s] * w[0]
    for j in range(1, k):
        out = out + xp[:, j:j + s] * w[j]
    return out


def mixer_c(h, w_in, conv_w, w_out):
    b_gate, c_gate, hv, z = jnp.split(h @ w_in, 4, axis=-1)
    y = b_gate * causal_depthwise_conv(c_gate * hv, conv_w)
    return (y * jax.nn.silu(z)) @ w_out


def _dense(key, fan_in, fan_out, scale=1.0):
    return jax.random.normal(key, (fan_in, fan_out), jnp.float32) * (scale * fan_in ** -0.5)


def _gain(key, n):
    return 1.0 + 0.1 * jax.random.normal(key, (n,), jnp.float32)


def _bias(key, n):
    return 0.02 * jax.random.normal(key, (n,), jnp.float32)


def setup_inputs(seed: int = 0) -> dict:
    key = jax.random.key(seed)
    keys = jax.random.split(key, 1 + DEPTH)
    params = {"x": jax.random.normal(keys[0], (BATCH, SEQ, D_MODEL), jnp.float32)}
    tags = ("a", "b", "c")
    for i in range(DEPTH):
        kind = i % N_MIXERS
        p = tags[kind] + str(i)
        k = jax.random.split(keys[1 + i], 8)
        if kind == 0:
            params[p + "_w_in"] = _dense(k[0], D_MODEL, 3 * D_INNER)
            params[p + "_v_gain"] = _gain(k[1], D_INNER)
            params[p + "_v_bias"] = _bias(k[2], D_INNER)
            params[p + "_w_s"] = jax.random.normal(k[3], (A_GROUPS, CHUNK, CHUNK), jnp.float32) * CHUNK ** -0.5
            params[p + "_b_s"] = 1.0 + 0.1 * jax.random.normal(k[4], (A_GROUPS, CHUNK), jnp.float32)
            params[p + "_w_out"] = _dense(k[5], D_INNER, D_MODEL, BETA)
        elif kind == 1:
            params[p + "_w_in"] = _dense(k[0], D_MODEL, 2 * D_INNER)
            params[p + "_w_grp"] = jax.random.normal(k[1], (B_GROUPS, B_GROUP_DIM, B_GROUP_DIM), jnp.float32) * B_GROUP_DIM ** -0.5
            params[p + "_scale"] = _gain(k[2], D_INNER)
            params[p + "_w_out"] = _dense(k[3], D_INNER, D_MODEL, BETA)
        else:
            params[p + "_w_in"] = _dense(k[0], D_MODEL, 4 * D_INNER)
            params[p + "_conv_w"] = jax.random.normal(k[1], (CONV_WIDTH, D_INNER), jnp.float32) * CONV_WIDTH ** -0.5
            params[p + "_w_out"] = _dense(k[2], D_INNER, D_MODEL, BETA)
        params["ln" + str(i) + "_gain"] = _gain(k[6], D_MODEL)
        params["ln" + str(i) + "_bias"] = _bias(k[7], D_MODEL)
    return params


def reference(x,
              a0_w_in, a0_v_gain, a0_v_bias, a0_w_s, a0_b_s, a0_w_out, ln0_gain, ln0_bias,
              b1_w_in, b1_w_grp, b1_scale, b1_w_out, ln1_gain, ln1_bias,
              c2_w_in, c2_conv_w, c2_w_out, ln2_gain, ln2_bias,
              a3_w_in, a3_v_gain, a3_v_bias, a3_w_s, a3_b_s, a3_w_out, ln3_gain, ln3_bias):
    mixers = (mixer_a, mixer_b, mixer_c)
    layer_params = [
        (a0_w_in, a0_v_gain, a0_v_bias, a0_w_s, a0_b_s, a0_w_out),
        (b1_w_in, b1_w_grp, b1_scale, b1_w_out),
        (c2_w_in, c2_conv_w, c2_w_out),
        (a3_w_in, a3_v_gain, a3_v_bias, a3_w_s, a3_b_s, a3_w_out),
    ]
    norm_params = [(ln0_gain, ln0_bias), (ln1_gain, ln1_bias), (ln2_gain, ln2_bias), (ln3_gain, ln3_bias)]
    for i in range(DEPTH):
        mixer = mixers[i % N_MIXERS]
        g, b = norm_params[i]
        x = layer_norm(ALPHA * x + mixer(x, *layer_params[i]), g, b)
    return x
```

```python
import numpy as np
import concourse.bass as bass
import concourse.mybir as mybir
from concourse.bass_utils import run_bass_kernel_spmd

F32 = mybir.dt.float32
BF16 = mybir.dt.bfloat16
AF = mybir.ActivationFunctionType
ALU = mybir.AluOpType

N_CORES = 8
D = 1024
DI = 2048
TOK = 4096
HALO = 128
TW = 512
ALPHA = float(8.0 ** 0.25)
EPS = 1e-5
POOL_W = (2, 4, 8, 16)
NSLOT = 4
NCONV = 6
SLOT = 4096
DBG_LAYERS = 4
PE_LABELS = []


class Buf:
    def __init__(self, name, ap, es):
        self.name = name
        self.ap = ap
        self.es = es
        self.nbytes = ap.shape[-1] * es
        self.segs = [[0, self.nbytes, None, {}]]

    def _split(self, pos):
        for i, sg in enumerate(self.segs):
            if sg[0] < pos < sg[1]:
                self.segs.insert(i + 1, [pos, sg[1], sg[2], dict(sg[3])])
                sg[1] = pos
                return

    def touch(self, lo, hi):
        assert 0 <= lo < hi <= self.nbytes, (self.name, lo, hi, self.nbytes)
        self._split(lo)
        self._split(hi)
        return [sg for sg in self.segs if sg[0] >= lo and sg[1] <= hi]

    def set_writer(self, lo, hi, ev):
        self.touch(lo, hi)
        keep = [sg for sg in self.segs if not (sg[0] >= lo and sg[1] <= hi)]
        keep.append([lo, hi, ev, {}])
        keep.sort(key=lambda s: s[0])
        self.segs = keep

    def v(self, a, b):
        return View(self, self.ap[:, a:b], [(a * self.es, b * self.es)])

    def all(self):
        return View(self, self.ap, [(0, self.nbytes)])


class View:
    def __init__(self, buf, ap, ivs):
        self.buf = buf
        self.ap = ap
        self.ivs = ivs

    def with_ap(self, ap):
        return View(self.buf, ap, self.ivs)


class Sched:
    ENGS = ("pe", "act", "dve", "pool", "sp")

    def __init__(self, nc):
        self.nc = nc
        self.prog = {e: [] for e in self.ENGS}
        self.sems = {}
        self.cnt = {}
        self.waited = {e: {} for e in self.ENGS}
        self.needed = {}
        for e in ("pe", "act", "dve", "pool"):
            self.newsem("E" + e)
            self.needed["E" + e] = set()

    def newsem(self, key):
        self.sems[key] = self.nc.alloc_semaphore(key)
        self.cnt[key] = 0
        return key

    def op(self, eng, fn, reads=(), writes=(), dma=None, ndma=1):
        need = {}

        def add(ev):
            if ev is None:
                return
            k, val = ev
            if need.get(k, 0) < val:
                need[k] = val

        own = None if dma is not None else "E" + eng
        for v in reads:
            for lo, hi in v.ivs:
                for sg in v.buf.touch(lo, hi):
                    add(sg[2])
        for v in writes:
            for lo, hi in v.ivs:
                for sg in v.buf.touch(lo, hi):
                    if sg[2] is not None and sg[2][0] != own:
                        add(sg[2])
                    for k, val in sg[3].items():
                        if k != own:
                            add((k, val))
        if eng == "pe" and own in need:
            del need[own]
        waits = []
        wd = self.waited[eng]
        for k, val in need.items():
            if wd.get(k, 0) < val:
                wd[k] = val
                waits.append((k, val))
                if k in self.needed:
                    self.needed[k].add(val)
        if dma is not None:
            key = dma
            self.cnt[key] += 16 * ndma
        else:
            key = own
            self.cnt[key] += 1
        ev = (key, self.cnt[key])
        self.prog[eng].append((waits, fn, key, dma is not None, ev[1]))
        for v in reads:
            for lo, hi in v.ivs:
                for sg in v.buf.touch(lo, hi):
                    if sg[3].get(key, 0) < ev[1]:
                        sg[3][key] = ev[1]
        for v in writes:
            for lo, hi in v.ivs:
                v.buf.set_writer(lo, hi, ev)
        return ev

    def final_wait(self, eng, keys):
        waits = [(k, self.cnt[k]) for k in keys if self.cnt[k] > 0]
        self.prog[eng].append((waits, None, None, False, 0))

    def emit(self):
        nc = self.nc
        sems = self.sems

        rank = {k: {idx: r + 1 for r, idx in enumerate(sorted(v))} for k, v in self.needed.items()}

        def mk(name):
            def body(e):
                for waits, fn, key, is_dma, idx in self.prog[name]:
                    for k, val in waits:
                        e.wait_ge(sems[k], rank[k][val] if k in rank else val)
                    if fn is None:
                        continue
                    r = fn(e)
                    if is_dma:
                        if not isinstance(r, (list, tuple)):
                            r = [r]
                        for ins in r:
                            ins.then_inc(sems[key], 16)
                    elif idx in rank[key]:
                        r.then_inc(sems[key], 1)
            return body

        with nc.Block() as block:
            block.tensor(mk("pe"))
            block.scalar(mk("act"))
            block.vector(mk("dve"))
            block.gpsimd(mk("pool"))
            block.sync(mk("sp"))


class Rot:
    def __init__(self, bufs):
        self.bufs = bufs
        self.i = 0

    def get(self):
        b = self.bufs[self.i % len(self.bufs)]
        self.i += 1
        return b


def build_program():
    nc = bass.Bass("TRN2", target_bir_lowering=False)
    S = Sched(nc)

    def din(name, shape):
        return nc.dram_tensor(name, list(shape), F32, kind="ExternalInput").ap()

    xin = din("xin", [HALO + TOK, D])
    yout = nc.dram_tensor("y", [TOK, D], F32, kind="ExternalOutput").ap()
    w_in = {0: din("w_in0", [D, 3 * DI]), 1: din("w_in1", [D, 2 * DI]),
            2: din("w_in2", [D, 4 * DI]), 3: din("w_in3", [D, 3 * DI])}
    w_out = {l: din("w_out%d" % l, [DI, D]) for l in range(4)}
    w_grp = din("w_grp", [4, 512, 512])
    cols_d = din("cols", [128, 129])
    wsT_d = din("wsT", [2, 128, 1024])
    bsbc_d = din("bsbc", [2, 128, 1024])
    mask_d = din("maskT", [128, 128])
    lnbc_d = din("lnbc", [4, 128, 2048])
    pm_d = din("pmats", [128, 20 * 128])
    ident_d = din("ident", [128, 128])

    def sb(name, ncols, dt):
        h = nc.alloc_sbuf_tensor("sb_" + name, [128, ncols], dt)
        return Buf(name, h.ap(), 4 if dt == F32 else 2)

    x_tm = sb("x_tm", 4 * D, F32)
    xT = sb("xT", 8 * TW, BF16)
    GT = sb("GT", 16 * TW, BF16)
    tm = sb("tm", 4 * DI, BF16)
    vcar = sb("vcar", DI, BF16)
    pooled = sb("pooled", 2 * 4 * TW, BF16)
    ring = sb("ring", NSLOT * SLOT, BF16)
    wout = sb("wout", 16 * D, BF16)
    lnbc = sb("lnbc", 2048, F32)
    xbn = sb("xbn", 4 * D, BF16)
    cols = sb("cols", 129, F32)
    cmat = sb("cmat", 2 * 16 * 128, F32)
    wct = sb("wct", 2 * 8 * 128, BF16)
    pm = sb("pm", 20 * 128, BF16)
    ident = sb("ident", 128, BF16)
    ones = sb("ones", 128, BF16)
    qcar = sb("qcar", 32, F32)
    mhalf = sb("mhalf", 1, F32)
    s2k = Rot([sb("s2k%d" % i, 516, F32) for i in range(8)])
    s4k = Rot([sb("s4k%d" % i, 1024, F32) for i in range(3)])
    small = sb("small", 512, F32)
    small_i = [0]

    def sm(n):
        if small_i[0] + n > 512:
            small_i[0] = 0
        a = small_i[0]
        small_i[0] += n
        return small.v(a, a + n)

    banks = []
    for i in range(8):
        h = nc.alloc_psum_tensor("pb%d" % i, [128, 512], F32)
        banks.append(Buf("pb%d" % i, h.ap(), 4))
    bank_rot = Rot(banks)

    def bf16view(buf, ncols):
        return View(buf, buf.ap.bitcast(BF16)[:, 0:ncols], [(0, ncols * 2)])

    for k in ["xld%d" % s for s in range(4)] + ["xst%d" % s for s in range(4)] + \
             ["xb%d" % s for s in range(4)] + \
             ["rg%d" % s for s in range(NSLOT)] + ["wo%d" % j for j in range(4)] + \
             ["ln0", "c0", "c1", "c2", "c3", "c4", "c5"] + ["cv%d" % j for j in range(NCONV)]:
        S.newsem(k)

    def dma(eng, sem, out_v, in_ap, reads=(), out_ap=None, n=1, fn=None):
        oap = out_v.ap if out_ap is None else out_ap
        if fn is None:
            def fn(e, oap=oap, in_ap=in_ap):
                return e.dma_start(out=oap, in_=in_ap)
        return S.op(eng, fn, reads=reads, writes=[out_v], dma=sem, ndma=n)

    dma("sp", "c0", cols.all(), cols_d)
    dma("pool", "c1", ident.all(), ident_d)
    dma("pool", "c2", pm.all(), pm_d)
    S.op("dve", lambda e: e.memset(ones.ap, 1.0), writes=[ones.all()])
    S.op("pool", lambda e: e.memset(vcar.ap, 0.0), writes=[vcar.all()])
    S.op("pool", lambda e: e.memset(qcar.ap, 0.0), writes=[qcar.all()])
    S.op("pool", lambda e: e.memset(mhalf.ap, -0.5), writes=[mhalf.all()])

    def col(i):
        return cols.v(i, i + 1)

    mk = s2k.get()
    mkv = mk.v(0, 128)
    dma("sp", "c3", mkv, mask_d)
    for a in range(2):
        wsb = s4k.get()
        bsb = s4k.get()
        dma("sp", "c4", wsb.all(), wsT_d[a])
        dma("sp", "c5", bsb.all(), bsbc_d[a])
        wv = wct.v(a * 1024, (a + 1) * 1024)
        S.op("dve",
             lambda e, o=wv.ap.rearrange("p (g t) -> p g t", g=8),
             i0=wsb.ap.rearrange("p (g t) -> p g t", g=8),
             i1=mkv.ap.unsqueeze(1).broadcast_to([128, 8, 128]):
             e.tensor_tensor(out=o, in0=i0, in1=i1, op=ALU.mult),
             reads=[wsb.all(), mkv], writes=[wv])
        rb = [bank_rot.get(), bank_rot.get()]
        for hh in range(2):
            def f(e, hh=hh, rb=rb, a=a):
                r = None
                for gg in range(4):
                    g = hh * 4 + gg
                    r = e.matmul(rb[hh].ap[:, gg * 128:(gg + 1) * 128], ones.ap,
                                 wct.ap[:, a * 1024 + g * 128: a * 1024 + (g + 1) * 128],
                                 start=True, stop=True)
                return r
            PE_LABELS.extend(["const"] * 4)
            S.op("pe", f, reads=[ones.all(), wv], writes=[rb[hh].all()])
        for c in range(16):
            g = c // 2
            rbv = rb[g // 4].v((g % 4) * 128, (g % 4 + 1) * 128)
            cv = cmat.v((a * 16 + c) * 128, (a * 16 + c + 1) * 128)
            vb = col(16 + 32 * a + c)
            bv = bsb.v(g * 128, (g + 1) * 128)
            S.op("dve", lambda e, o=cv.ap, i0=rbv.ap, sc=vb.ap, i1=bv.ap:
                 e.scalar_tensor_tensor(out=o, in0=i0, scalar=sc, in1=i1, op0=ALU.mult, op1=ALU.add),
                 reads=[rbv, vb, bv], writes=[cv])

    def layer_stages(l):
        if l in (0, 3):
            return [(l, k, j) for k in ("v", "u", "z") for j in range(4)]
        if l == 1:
            st = [(l, "v", j) for j in range(4)]
            for g in range(4):
                st += [(l, "grp", g), (l, "z", g)]
            return st
        return [(l, "c4", c) for c in range(16)]

    canon = []
    for l in range(4):
        canon += [(l, "wo", j) for j in range(4)]
        canon += layer_stages(l)
    sidx = {d: i for i, d in enumerate(canon)}
    NST = len(canon)
    wsc_t = nc.dram_tensor("wsc", [NST, 128, SLOT], BF16, kind="Internal")
    wsc_ap = wsc_t.ap()

    class DBuf(Buf):
        def __init__(self, name, nbytes):
            self.name = name
            self.nbytes = nbytes
            self.segs = [[0, nbytes, None, {}]]

    wsc = DBuf("wsc", NST * 128 * SLOT * 2)
    cvdummy = [DBuf("cvd%d" % j, 4) for j in range(NCONV)]

    def wsc_v(i):
        return View(wsc, wsc_ap[i], [(i * 128 * SLOT * 2, (i + 1) * 128 * SLOT * 2)])

    def stage_pairs(desc, dst):
        l, kind, idx = desc
        if kind == "wo":
            return [(dst.rearrange("p (c m) -> p c m", c=4),
                     w_out[l][idx * 512:(idx + 1) * 512, :].rearrange("(c p) m -> p c m", p=128))]
        w = w_in[l]
        if kind in ("v", "u", "z"):
            if l in (0, 3):
                base = {"u": 0, "v": DI, "z": 2 * DI}[kind]
            else:
                base = {"v": 0, "z": DI}[kind]
            c0 = base + idx * 512
            return [(dst.rearrange("p (d f) -> p d f", d=8),
                     w[:, c0:c0 + 512].rearrange("(d p) f -> p d f", p=128))]
        if kind == "grp":
            return [(dst[:, 0:2048].rearrange("p (d e) -> p d e", d=4),
                     w_grp[idx].rearrange("(d p) e -> p d e", p=128))]
        if kind == "c4":
            d4 = dst.rearrange("p (d b f) -> p d b f", d=8, b=4)
            return [(d4[:, :, b, :],
                     w[:, b * DI + idx * 128: b * DI + (idx + 1) * 128].rearrange("(d p) f -> p d f", p=128))
                    for b in range(4)]
        raise ValueError(kind)

    conv_state = {"n": 0}

    def ensure_conv(upto):
        while conv_state["n"] < min(NST, upto):
            i = conv_state["n"]
            pairs = stage_pairs(canon[i], wsc_ap[i])
            j = i % NCONV

            def fn(e, pairs=pairs):
                return [e.dma_start(out=o, in_=i_) for o, i_ in pairs]
            S.op("pool", fn, writes=[wsc_v(i), View(cvdummy[j], None, [(0, 4)])], dma="cv%d" % j, ndma=len(pairs))
            conv_state["n"] = i + 1

    plan = []
    ring_state = {"issued": 0, "consumed": 0}

    def ring_issue():
        i = ring_state["issued"]
        if i >= len(plan):
            return
        slot = i % NSLOT
        sv = ring.v(slot * SLOT, (slot + 1) * SLOT)
        ensure_conv(sidx[plan[i]] + NCONV)
        src = wsc_v(sidx[plan[i]])
        dma("sp", "rg%d" % slot, sv, src.ap, reads=[src])
        ring_state["issued"] = i + 1

    def ring_next(desc):
        i = ring_state["consumed"]
        assert plan[i] == desc, (plan[i], desc)
        assert i < ring_state["issued"]
        slot = i % NSLOT
        return slot * SLOT

    def ring_next2(desc):
        i = ring_state["consumed"] + 1
        assert plan[i] == desc, (plan[i], desc)
        assert i < ring_state["issued"]
        return (i % NSLOT) * SLOT

    def ring_release():
        ring_state["consumed"] += 1
        ring_issue()

    tiles = [(TW * i, 4) for i in range(6)] + [(3072 + 384 * i, 3) for i in range(3)]
    assert tiles[-1][0] + 128 * tiles[-1][1] == HALO + TOK
    for tok0, NS in tiles:
        for l in range(DBG_LAYERS):
            plan.extend(layer_stages(l))

    def bank():
        return bank_rot.get()

    cur_label = ["init"]

    def mm_group(out_v, pairs, reads, first=True, last=True):
        def f(e, o=out_v.ap, pairs=pairs):
            r = None
            n = len(pairs)
            for i, (l_, r_) in enumerate(pairs):
                r = e.matmul(o, l_, r_, start=(first and i == 0), stop=(last and i == n - 1))
            return r
        PE_LABELS.extend([cur_label[0]] * len(pairs))
        S.op("pe", f, reads=reads, writes=[out_v])

    def xT_v(d, t0, t1):
        return xT.v(d * TW + t0, d * TW + t1)

    def xT_all(t0, t1):
        return View(xT, None, [((d * TW + t0) * 2, (d * TW + t1) * 2) for d in range(8)])

    def xT_build(s, xb):
        tb = bank()
        tv = tb.all()
        tb_bf = tb.ap.bitcast(BF16)

        def f(e, xb=xb, tb_bf=tb_bf):
            r = None
            for c in range(8):
                r = e.transpose(tb_bf[:, c * 128:(c + 1) * 128], xb.ap[:, c * 128:(c + 1) * 128], ident.ap)
            return r
        PE_LABELS.extend(["T"] * 8)
        S.op("pe", f, reads=[xb, ident.all()], writes=[tv])
        ov = xT_all(s * 128, (s + 1) * 128)
        oap = xT.ap.rearrange("p (d t) -> p d t", d=8)[:, :, s * 128:(s + 1) * 128]
        iap = tb_bf.rearrange("p (d t) -> p d t", d=8)
        S.op("act", lambda e, o=oap, i=iap: e.activation(out=o, in_=i, func=AF.Copy),
             reads=[tv], writes=[ov])

    def ln_small(stats_v, nst):
        mv = sm(2)
        S.op("dve", lambda e, o=mv.ap, i=stats_v.ap: e.bn_aggr(out=o, in_=i), reads=[stats_v], writes=[mv])
        ve = sm(1)
        S.op("dve", lambda e, o=ve.ap, i=mv.ap[:, 1:2]:
             e.tensor_scalar(out=o, in0=i, scalar1=EPS, scalar2=None, op0=ALU.add),
             reads=[mv], writes=[ve])
        rstd = sm(1)
        S.op("pool", lambda e, o=rstd.ap, i=ve.ap, h=mhalf.ap: e.tensor_tensor(out=o, in0=i, in1=h, op=ALU.pow),
             reads=[ve, mhalf.all()], writes=[rstd])
        nmr = sm(1)
        S.op("dve", lambda e, o=nmr.ap, i=mv.ap[:, 0:1], r=rstd.ap:
             e.scalar_tensor_tensor(out=o, in0=i, scalar=-1.0, in1=r, op0=ALU.mult, op1=ALU.mult),
             reads=[mv, rstd], writes=[nmr])
        return rstd, nmr

    first_ln = [True]

    def start_layer_loads(l):
        ensure_conv(sidx[(l, "wo", 3)] + NCONV)
        for j in range(4):
            ov = wout.v(j * 4096, (j + 1) * 4096)
            src = wsc_v(sidx[(l, "wo", j)])
            dma("sp", "wo%d" % j, ov, src.ap, reads=[src])
        if first_ln[0]:
            first_ln[0] = False
        else:
            dma("pool", "ln0", lnbc.all(), lnbc_d[l])

    def wout_and_epilogue(l, NS, tok0, final, next_ns=0, skip0=False):
        cur_label[0] = "L%d:wout" % l
        gv = lnbc.v(0, 1024)
        bv = lnbc.v(1024, 2048)
        xbs = []
        for s in range(NS):
            if final and skip0 and s == 0:
                if s < next_ns:
                    xT_build(s, xbn.v(s * D, (s + 1) * D))
                continue
            bs_ = [bank(), bank()]
            for h in range(2):
                for (c0, c1) in ((0, 12), (12, 16)):
                    pairs = [(GT.ap[:, c * TW + s * 128: c * TW + (s + 1) * 128],
                              wout.ap[:, c * 1024 + h * 512: c * 1024 + (h + 1) * 512]) for c in range(c0, c1)]
                    rd = [GT.v(c * TW + s * 128, c * TW + (s + 1) * 128) for c in range(c0, c1)] + \
                         [wout.v(c0 * 1024, c1 * 1024)]
                    mm_group(bs_[h].all(), pairs, rd, first=(c0 == 0), last=(c1 == 16))
            if final and s < next_ns:
                xT_build(s, xbn.v(s * D, (s + 1) * D))
            sbuf_ = s4k.get()
            xs = x_tm.v(s * D, (s + 1) * D)
            st = sm(12)
            for h in range(2):
                sv = sbuf_.v(h * 512, (h + 1) * 512)
                xh = x_tm.v(s * D + h * 512, s * D + (h + 1) * 512)
                S.op("dve", lambda e, o=sv.ap, i0=xh.ap, i1=bs_[h].ap:
                     e.scalar_tensor_tensor(out=o, in0=i0, scalar=ALPHA, in1=i1, op0=ALU.mult, op1=ALU.add),
                     reads=[xh, bs_[h].all()], writes=[sv])
                stv = View(small, st.ap[:, h * 6:(h + 1) * 6], [(st.ivs[0][0] + h * 24, st.ivs[0][0] + (h + 1) * 24)])
                S.op("dve", lambda e, o=stv.ap, i=sv.ap: e.bn_stats(out=o, in_=i), reads=[sv], writes=[stv])
            rstd, nmr = ln_small(st, 2)
            sa = sbuf_.all()
            S.op("act", lambda e, o=sa.ap, r=rstd.ap, n=nmr.ap:
                 e.activation(out=o, in_=o, func=AF.Identity, bias=n, scale=r),
                 reads=[sa, rstd, nmr], writes=[sa])
            S.op("dve", lambda e, o=sa.ap, g=gv.ap: e.tensor_tensor(out=o, in0=o, in1=g, op=ALU.mult),
                 reads=[sa, gv], writes=[sa])
            if final:
                yo = s4k.get().all()
                S.op("dve", lambda e, o=yo.ap, i=sa.ap, b=bv.ap: e.tensor_tensor(out=o, in0=i, in1=b, op=ALU.add),
                     reads=[sa, bv], writes=[yo])
                r0 = tok0 - HALO + s * 128
                assert 0 <= r0 <= TOK - 128
                S.op("pool", lambda e, o=yout[r0:r0 + 128, :], i=yo.ap: e.dma_start(out=o, in_=i),
                     reads=[yo], dma="xst%d" % s)
            else:
                xb = bf16view(s2k.get(), 1024)
                S.op("dve", lambda e, o=xb.ap, i=sa.ap, b=bv.ap: e.tensor_tensor(out=o, in0=i, in1=b, op=ALU.add),
                     reads=[sa, bv], writes=[xb])
                S.op("pool", lambda e, o=xs.ap, i=sa.ap, b=bv.ap: e.tensor_tensor(out=o, in0=i, in1=b, op=ALU.add),
                     reads=[sa, bv], writes=[xs])
                xbs.append((s, xb))
        if final:
            return None
        for s, xb in xbs[:-1]:
            xT_build(s, xb)
        return xbs[-1]

    def v_part(l, NS, func, with_stats, pend=None):
        cur_label[0] = "L%d:v" % l
        stv = [sm(24) for _ in range(NS)] if with_stats else None
        for j in range(4):
            so = ring_next((l, "v", j))
            for s in range(NS):
                if pend is not None and s == pend[0]:
                    xT_build(*pend)
                    pend = None
                b = bank()
                pairs = [(xT.ap[:, d * TW + s * 128: d * TW + (s + 1) * 128],
                          ring.ap[:, so + d * 512: so + (d + 1) * 512]) for d in range(8)]
                mm_group(b.all(), pairs, [xT_all(s * 128, (s + 1) * 128), ring.v(so, so + SLOT)])
                tv = tm.v(s * DI + j * 512, s * DI + (j + 1) * 512)
                S.op("act", lambda e, o=tv.ap, i=b.ap, func=func: e.activation(out=o, in_=i, func=func),
                     reads=[b.all()], writes=[tv])
                if with_stats:
                    lo = stv[s].ivs[0][0]
                    sv = View(small, stv[s].ap[:, j * 6:(j + 1) * 6], [(lo + j * 24, lo + (j + 1) * 24)])
                    S.op("dve", lambda e, o=sv.ap, i=tv.ap: e.bn_stats(out=o, in_=i), reads=[tv], writes=[sv])
            ring_release()
        return stv

    def fm_group(so, k, W, stride=512, boff=0):
        b = bank()
        pairs = [(ring.ap[:, so + d * stride + boff + k * 128: so + d * stride + boff + (k + 1) * 128],
                  xT.ap[:, d * TW: d * TW + W]) for d in range(8)]
        mm_group(b.v(0, W), pairs, [xT_all(0, W), ring.v(so, so + SLOT)])
        return b

    def layer_A(l, a, NS, W, tok0, final, next_ns=0, pend=None, skip0=False):
        start_layer_loads(l)
        stv = v_part(l, NS, AF.Gelu_apprx_tanh, True, pend)
        cur_label[0] = "L%d:uz" % l
        for s in range(NS):
            rstd, nmr = ln_small(stv[s], 4)
            tv = tm.v(s * DI, (s + 1) * DI)
            S.op("dve", lambda e, o=tv.ap, r=rstd.ap, n=nmr.ap:
                 e.tensor_scalar(out=o, in0=o, scalar1=r, scalar2=n, op0=ALU.mult, op1=ALU.add),
                 reads=[tv, rstd, nmr], writes=[tv])
        for j in range(4):
            so = ring_next((l, "u", j))
            for k in range(4):
                c = 4 * j + k
                b = fm_group(so, k, W)
                gv_ = GT.v(c * TW, c * TW + W)
                S.op("act", lambda e, o=gv_.ap, i=b.ap[:, 0:W]: e.activation(out=o, in_=i, func=AF.Gelu_apprx_tanh),
                     reads=[b.v(0, W)], writes=[gv_])
            ring_release()
        for j in range(4):
            so = ring_next((l, "z", j))
            for k in range(4):
                c = 4 * j + k
                g = c // 2
                bz = fm_group(so, k, W)
                szb = s2k.get()
                sz = bf16view(szb, W)
                S.op("act", lambda e, o=sz.ap, i=bz.ap[:, 0:W]: e.activation(out=o, in_=i, func=AF.Silu),
                     reads=[bz.v(0, W)], writes=[sz])
                bs_ = bank()
                wv = wct.v(a * 1024 + g * 128, a * 1024 + (g + 1) * 128)

                def f(e, bs_=bs_, c=c, wv=wv):
                    r = None
                    for s in range(NS):
                        r = e.matmul(bs_.ap[:, s * 128:(s + 1) * 128],
                                     tm.ap[:, s * DI + c * 128: s * DI + (c + 1) * 128], wv.ap,
                                     start=True, stop=True)
                    return r
                PE_LABELS.extend([cur_label[0] + ":spat"] * NS)
                S.op("pe", f, reads=[tm.v(s * DI + c * 128, s * DI + (c + 1) * 128) for s in range(NS)] + [wv],
                     writes=[bs_.v(0, W)])
                tb = s2k.get()
                tmp = tb.v(0, W)
                cv = cmat.v((a * 16 + c) * 128, (a * 16 + c + 1) * 128)
                gc = col(32 * a + c)
                S.op("dve", lambda e, o=tmp.ap.rearrange("p (s t) -> p s t", s=NS),
                     i0=bs_.ap[:, 0:W].rearrange("p (s t) -> p s t", s=NS), sc=gc.ap,
                     i1=cv.ap.unsqueeze(1).broadcast_to([128, NS, 128]):
                     e.scalar_tensor_tensor(out=o, in0=i0, scalar=sc, in1=i1, op0=ALU.mult, op1=ALU.add),
                     reads=[bs_.v(0, W), gc, cv], writes=[tmp])
                S.op("dve", lambda e, o=tmp.ap, i1=sz.ap: e.tensor_tensor(out=o, in0=o, in1=i1, op=ALU.mult),
                     reads=[tmp, sz], writes=[tmp])
                gv_ = GT.v(c * TW, c * TW + W)
                S.op("dve", lambda e, o=gv_.ap, i1=tmp.ap: e.tensor_tensor(out=o, in0=o, in1=i1, op=ALU.mult),
                     reads=[gv_, tmp], writes=[gv_])
            ring_release()
        return wout_and_epilogue(l, NS, tok0, final, next_ns, skip0)

    def layer_B(l, NS, W, tok0, special_s, final=False, next_ns=0, pend=None, skip0=False):
        start_layer_loads(l)
        v_part(l, NS, AF.Copy, False, pend)
        cur_label[0] = "L%d:pz" % l
        for g in range(4):
            so_g = ring_next((l, "grp", g))
            pset = g % 2
            for k in range(4):
                c = 4 * g + k
                bp = bank()

                def f(e, bp=bp, c=c, g=g):
                    r = None
                    for s in range(NS):
                        cur = tm.ap[:, s * DI + c * 128: s * DI + (c + 1) * 128]
                        prv = (tm.ap[:, (s - 1) * DI + c * 128:(s - 1) * DI + (c + 1) * 128] if s > 0
                               else vcar.ap[:, c * 128:(c + 1) * 128])
                        if special_s is not None and s == special_s:
                            ms = [(cur, 8 + g), (cur, 12 + g), (prv, 16 + g)]
                        else:
                            ms = [(cur, g), (prv, 4 + g)]
                        for i, (lt, mi) in enumerate(ms):
                            r = e.matmul(bp.ap[:, s * 128:(s + 1) * 128], lt, pm.ap[:, mi * 128:(mi + 1) * 128],
                                         start=(i == 0), stop=(i == len(ms) - 1))
                    return r
                PE_LABELS.extend(["L1:pool"] * (2 * NS + (1 if special_s is not None else 0)))
                S.op("pe", f,
                     reads=[tm.v(s * DI + c * 128, s * DI + (c + 1) * 128) for s in range(NS)] +
                           [vcar.v(c * 128, (c + 1) * 128), pm.all()],
                     writes=[bp.v(0, W)])
                pv = pooled.v((pset * 4 + k) * TW, (pset * 4 + k) * TW + W)
                S.op("dve", lambda e, o=pv.ap, i=bp.ap[:, 0:W]: e.tensor_copy(out=o, in_=i),
                     reads=[bp.v(0, W)], writes=[pv])
            so_z = ring_next2((l, "z", g))
            for k2 in range(4):
                ec = 4 * g + k2
                bm = bank()
                pairs = [(ring.ap[:, so_g + dk * 512 + k2 * 128: so_g + dk * 512 + (k2 + 1) * 128],
                          pooled.ap[:, (pset * 4 + dk) * TW:(pset * 4 + dk) * TW + W]) for dk in range(4)]
                mm_group(bm.v(0, W), pairs,
                         [ring.v(so_g, so_g + 2048)] +
                         [pooled.v((pset * 4 + dk) * TW, (pset * 4 + dk) * TW + W) for dk in range(4)])
                bz = fm_group(so_z, k2, W)
                szb = s2k.get()
                sz = bf16view(szb, W)
                S.op("act", lambda e, o=sz.ap, i=bz.ap[:, 0:W]: e.activation(out=o, in_=i, func=AF.Silu),
                     reads=[bz.v(0, W)], writes=[sz])
                gv_ = GT.v(ec * TW, ec * TW + W)
                sc = col(64 + ec)
                S.op("dve", lambda e, o=gv_.ap, i0=bm.ap[:, 0:W], sc=sc.ap, i1=sz.ap:
                     e.scalar_tensor_tensor(out=o, in0=i0, scalar=sc, in1=i1, op0=ALU.mult, op1=ALU.mult),
                     reads=[bm.v(0, W), sc, sz], writes=[gv_])
            ring_release()
            ring_release()
        lastv = tm.v((NS - 1) * DI, NS * DI)
        S.op("pool", lambda e, o=vcar.ap, i=lastv.ap: e.tensor_copy(out=o, in_=i),
             reads=[lastv], writes=[vcar.all()])
        return wout_and_epilogue(l, NS, tok0, final, next_ns, skip0)

    def layer_C(l, NS, W, tok0, halo0, final=False, next_ns=0, pend=None, skip0=False):
        start_layer_loads(l)
        if pend is not None:
            xT_build(*pend)
        cur_label[0] = "L%d:c" % l
        for c in range(16):
            so = ring_next((l, "c4", c))
            bb = fm_group(so, 0, W, boff=0)
            bc = fm_group(so, 0, W, boff=128)
            bh = fm_group(so, 0, W, boff=256)
            bz = fm_group(so, 0, W, boff=384)
            ring_release()
            csb = s2k.get()
            cs = csb.v(0, W)
            S.op("act", lambda e, o=cs.ap, i=bc.ap[:, 0:W]: e.activation(out=o, in_=i, func=AF.Copy),
                 reads=[bc.v(0, W)], writes=[cs])
            qb = s2k.get()
            qc = qcar.v(2 * c, 2 * c + 2)
            q0 = qb.v(0, 2)
            S.op("pool", lambda e, o=q0.ap, i=qc.ap: e.tensor_copy(out=o, in_=i), reads=[qc], writes=[q0])
            qm = qb.v(2, 2 + W)
            S.op("dve", lambda e, o=qm.ap, i0=bh.ap[:, 0:W], i1=cs.ap: e.tensor_tensor(out=o, in0=i0, in1=i1, op=ALU.mult),
                 reads=[bh.v(0, W), cs], writes=[qm])
            if halo0:
                qh = qb.v(2, 2 + 128)
                fl = col(128)
                S.op("dve", lambda e, o=qh.ap, f_=fl.ap:
                     e.tensor_scalar(out=o, in0=o, scalar1=f_, scalar2=None, op0=ALU.mult),
                     reads=[qh, fl], writes=[qh])
            qt = qb.v(W, W + 2)
            S.op("pool", lambda e, o=qc.ap, i=qt.ap: e.tensor_copy(out=o, in_=i), reads=[qt], writes=[qc])
            szb = s2k.get()
            sz = bf16view(szb, W)
            S.op("act", lambda e, o=sz.ap, i=bz.ap[:, 0:W]: e.activation(out=o, in_=i, func=AF.Silu),
                 reads=[bz.v(0, W)], writes=[sz])
            tb = s2k.get()
            t = tb.v(0, W)
            w0, w1, w2 = col(80 + c), col(96 + c), col(112 + c)
            S.op("dve", lambda e, o=t.ap, i=qm.ap, sc=w2.ap:
                 e.tensor_scalar(out=o, in0=i, scalar1=sc, scalar2=None, op0=ALU.mult),
                 reads=[qm, w2], writes=[t])
            q1 = qb.v(1, 1 + W)
            S.op("dve", lambda e, o=t.ap, i=q1.ap, sc=w1.ap:
                 e.scalar_tensor_tensor(out=o, in0=i, scalar=sc, in1=o, op0=ALU.mult, op1=ALU.add),
                 reads=[q1, w1, t], writes=[t])
            q2 = qb.v(0, W)
            S.op("dve", lambda e, o=t.ap, i=q2.ap, sc=w0.ap:
                 e.scalar_tensor_tensor(out=o, in0=i, scalar=sc, in1=o, op0=ALU.mult, op1=ALU.add),
                 reads=[q2, w0, t], writes=[t])
            S.op("dve", lambda e, o=t.ap, i1=bb.ap[:, 0:W]: e.tensor_tensor(out=o, in0=o, in1=i1, op=ALU.mult),
                 reads=[t, bb.v(0, W)], writes=[t])
            gv_ = GT.v(c * TW, c * TW + W)
            S.op("dve", lambda e, o=gv_.ap, i0=t.ap, i1=sz.ap: e.tensor_tensor(out=o, in0=i0, in1=i1, op=ALU.mult),
                 reads=[t, sz], writes=[gv_])
        return wout_and_epilogue(l, NS, tok0, final, next_ns, skip0)

    def load_xbn(tok0, NS):
        for s in range(NS):
            dma("pool", "xb%d" % s, xbn.v(s * D, (s + 1) * D), xin[tok0 + s * 128: tok0 + (s + 1) * 128, :])

    load_xbn(*tiles[0])
    for s in range(tiles[0][1]):
        dma("pool", "xld%d" % s, x_tm.v(s * D, (s + 1) * D), xin[tiles[0][0] + s * 128: tiles[0][0] + (s + 1) * 128, :])
    dma("pool", "ln0", lnbc.all(), lnbc_d[0])
    ensure_conv(NCONV)
    for _ in range(NSLOT):
        ring_issue()
    for s in range(tiles[0][1]):
        xT_build(s, xbn.v(s * D, (s + 1) * D))
    for ti, (tok0, NS) in enumerate(tiles):
        W = NS * 128
        t0 = ti == 0
        nxt = tiles[ti + 1] if ti + 1 < len(tiles) else None
        if not t0:
            for s in range(NS):
                xs = x_tm.v(s * D, (s + 1) * D)
                dma("pool", "xld%d" % s, xs, xin[tok0 + s * 128: tok0 + (s + 1) * 128, :])
        if nxt is not None:
            load_xbn(*nxt)
        next_ns = nxt[1] if nxt is not None else 0
        pend = None
        for l in range(DBG_LAYERS):
            final = l == DBG_LAYERS - 1
            nn = next_ns if final else 0
            if l in (0, 3):
                pend = layer_A(l, 0 if l == 0 else 1, NS, W, tok0, final, nn, pend, t0)
            elif l == 1:
                pend = layer_B(l, NS, W, tok0, 1 if t0 else None, final, nn, pend, t0)
            else:
                pend = layer_C(l, NS, W, tok0, t0, final, nn, pend, t0)
    assert ring_state["consumed"] == len(plan)
    S.final_wait("sp", ["xst%d" % s for s in range(4)])
    S.emit()
    return nc


def _colvec(v):
    return np.ascontiguousarray(np.asarray(v, np.float32).reshape(16, 128).T)


def _pool_mats(second_half):
    m = np.zeros((20, 128, 128), np.float32)
    s = np.arange(128)[:, None]
    t = np.arange(128)[None, :]
    eye = (s == t).astype(np.float32)
    for g, w in enumerate(POOL_W):
        band = ((t - s >= 0) & (t - s <= w - 1)).astype(np.float32)
        main = band / w - eye
        prev = ((t + 128 - s) <= w - 1).astype(np.float32) / w
        m[g] = main
        m[4 + g] = prev
        if second_half:
            m[8 + g] = main
            m[16 + g] = prev
        else:
            cnt = np.minimum(np.arange(128) + 1, w).astype(np.float32)[None, :]
            start = band / cnt - eye
            import ml_dtypes
            hi = start.astype(ml_dtypes.bfloat16).astype(np.float32)
            m[8 + g] = hi
            m[12 + g] = start - hi
    return np.ascontiguousarray(m.transpose(1, 0, 2).reshape(128, 20 * 128))


_NC_CACHE = {}


def kernel(**inputs):
    f = lambda k: np.asarray(inputs[k], np.float32)
    x = f("x")
    B, SEQ, _ = x.shape
    assert (B, SEQ) == (4, 8192)
    cols_base = np.zeros((128, 129), np.float32)
    cols_base[:, 0:16] = _colvec(f("a0_v_gain"))
    cols_base[:, 16:32] = _colvec(f("a0_v_bias"))
    cols_base[:, 32:48] = _colvec(f("a3_v_gain"))
    cols_base[:, 48:64] = _colvec(f("a3_v_bias"))
    cols_base[:, 64:80] = _colvec(f("b1_scale"))
    cw = f("c2_conv_w")
    for j in range(3):
        cols_base[:, 80 + 16 * j: 96 + 16 * j] = _colvec(cw[j])
    wsT = np.stack([np.ascontiguousarray(f(k).transpose(2, 0, 1)).reshape(128, 1024)
                    for k in ("a0_w_s", "a3_w_s")])
    bsbc = np.stack([np.ascontiguousarray(np.broadcast_to(f(k).reshape(1, 1024), (128, 1024)))
                     for k in ("a0_b_s", "a3_b_s")])
    s_ = np.arange(128)[:, None]
    t_ = np.arange(128)[None, :]
    maskT = (s_ <= t_).astype(np.float32)
    lnbc = np.stack([np.concatenate([np.broadcast_to(f("ln%d_gain" % l)[None, :], (128, 1024)),
                                     np.broadcast_to(f("ln%d_bias" % l)[None, :], (128, 1024))], axis=1)
                     for l in range(4)]).astype(np.float32)
    ident = np.eye(128, dtype=np.float32)
    shared = {
        "w_in0": f("a0_w_in"), "w_in1": f("b1_w_in"), "w_in2": f("c2_w_in"), "w_in3": f("a3_w_in"),
        "w_out0": f("a0_w_out"), "w_out1": f("b1_w_out"), "w_out2": f("c2_w_out"), "w_out3": f("a3_w_out"),
        "w_grp": f("b1_w_grp"), "wsT": wsT, "bsbc": bsbc, "maskT": maskT, "lnbc": np.ascontiguousarray(lnbc),
        "ident": ident,
    }
    in_maps = []
    for i in range(N_CORES):
        b, h = i // 2, i % 2
        main = x[b, h * TOK:(h + 1) * TOK]
        halo = x[b, TOK - HALO:TOK] if h == 1 else x[b, 0:HALO]
        cols = cols_base.copy()
        cols[:, 128] = float(h)
        m = dict(shared)
        m["xin"] = np.ascontiguousarray(np.concatenate([halo, main], axis=0))
        m["cols"] = cols
        m["pmats"] = _pool_mats(h == 1)
        in_maps.append(m)
    if DBG_LAYERS not in _NC_CACHE:
        _NC_CACHE[DBG_LAYERS] = build_program()
    nc = _NC_CACHE[DBG_LAYERS]
    res = run_bass_kernel_spmd(nc, in_maps, core_ids=list(range(N_CORES)))
    out = np.empty((B, SEQ, D), np.float32)
    for i in range(N_CORES):
        b, h = i // 2, i % 2
        out[b, h * TOK:(h + 1) * TOK] = np.asarray(res.results[i]["y"], np.float32)
    return out
```

```python
import numpy as np
import concourse.bass as bass
import concourse.mybir as mybir
from concourse.bass_utils import run_bass_kernel_spmd

F32 = mybir.dt.float32
BF16 = mybir.dt.bfloat16
AF = mybir.ActivationFunctionType
ALU = mybir.AluOpType

N_CORES = 8
D = 1024
DI = 2048
TOK = 4096
HALO = 128
TW = 512
ALPHA = float(8.0 ** 0.25)
EPS = 1e-5
POOL_W = (2, 4, 8, 16)
NSLOT = 4
NCONV = 6
SLOT = 4096
DBG_LAYERS = 4
PE_LABELS = []


class Buf:
    def __init__(self, name, ap, es):
        self.name = name
        self.ap = ap
        self.es = es
        self.nbytes = ap.shape[-1] * es
        self.segs = [[0, self.nbytes, None, {}]]

    def _split(self, pos):
        for i, sg in enumerate(self.segs):
            if sg[0] < pos < sg[1]:
                self.segs.insert(i + 1, [pos, sg[1], sg[2], dict(sg[3])])
                sg[1] = pos
                return

    def touch(self, lo, hi):
        assert 0 <= lo < hi <= self.nbytes, (self.name, lo, hi, self.nbytes)
        self._split(lo)
        self._split(hi)
        return [sg for sg in self.segs if sg[0] >= lo and sg[1] <= hi]

    def set_writer(self, lo, hi, ev):
        self.touch(lo, hi)
        keep = [sg for sg in self.segs if not (sg[0] >= lo and sg[1] <= hi)]
        keep.append([lo, hi, ev, {}])
        keep.sort(key=lambda s: s[0])
        self.segs = keep

    def v(self, a, b):
        return View(self, self.ap[:, a:b], [(a * self.es, b * self.es)])

    def all(self):
        return View(self, self.ap, [(0, self.nbytes)])


class View:
    def __init__(self, buf, ap, ivs):
        self.buf = buf
        self.ap = ap
        self.ivs = ivs

    def with_ap(self, ap):
        return View(self.buf, ap, self.ivs)


class Sched:
    ENGS = ("pe", "act", "dve", "pool", "sp")

    def __init__(self, nc):
        self.nc = nc
        self.prog = {e: [] for e in self.ENGS}
        self.sems = {}
        self.cnt = {}
        self.waited = {e: {} for e in self.ENGS}
        self.needed = {}
        for e in ("pe", "act", "dve", "pool"):
            self.newsem("E" + e)
            self.needed["E" + e] = set()

    def newsem(self, key):
        self.sems[key] = self.nc.alloc_semaphore(key)
        self.cnt[key] = 0
        return key

    def op(self, eng, fn, reads=(), writes=(), dma=None, ndma=1):
        need = {}

        def add(ev):
            if ev is None:
                return
            k, val = ev
            if need.get(k, 0) < val:
                need[k] = val

        own = None if dma is not None else "E" + eng
        for v in reads:
            for lo, hi in v.ivs:
                for sg in v.buf.touch(lo, hi):
                    add(sg[2])
        for v in writes:
            for lo, hi in v.ivs:
                for sg in v.buf.touch(lo, hi):
                    if sg[2] is not None and sg[2][0] != own:
                        add(sg[2])
                    for k, val in sg[3].items():
                        if k != own:
                            add((k, val))
        if eng == "pe" and own in need:
            del need[own]
        waits = []
        wd = self.waited[eng]
        for k, val in need.items():
            if wd.get(k, 0) < val:
                wd[k] = val
                waits.append((k, val))
                if k in self.needed:
                    self.needed[k].add(val)
        if dma is not None:
            key = dma
            self.cnt[key] += 16 * ndma
        else:
            key = own
            self.cnt[key] += 1
        ev = (key, self.cnt[key])
        self.prog[eng].append((waits, fn, key, dma is not None, ev[1]))
        for v in reads:
            for lo, hi in v.ivs:
                for sg in v.buf.touch(lo, hi):
                    if sg[3].get(key, 0) < ev[1]:
                        sg[3][key] = ev[1]
        for v in writes:
            for lo, hi in v.ivs:
                v.buf.set_writer(lo, hi, ev)
        return ev

    def final_wait(self, eng, keys):
        waits = [(k, self.cnt[k]) for k in keys if self.cnt[k] > 0]
        self.prog[eng].append((waits, None, None, False, 0))

    def emit(self):
        nc = self.nc
        sems = self.sems

        rank = {k: {idx: r + 1 for r, idx in enumerate(sorted(v))} for k, v in self.needed.items()}

        def mk(name):
            def body(e):
                for waits, fn, key, is_dma, idx in self.prog[name]:
                    for k, val in waits:
                        e.wait_ge(sems[k], rank[k][val] if k in rank else val)
                    if fn is None:
                        continue
                    r = fn(e)
                    if is_dma:
                        if not isinstance(r, (list, tuple)):
                            r = [r]
                        for ins in r:
                            ins.then_inc(sems[key], 16)
                    elif idx in rank[key]:
                        r.then_inc(sems[key], 1)
            return body

        with nc.Block() as block:
            block.tensor(mk("pe"))
            block.scalar(mk("act"))
            block.vector(mk("dve"))
            block.gpsimd(mk("pool"))
            block.sync(mk("sp"))


class Rot:
    def __init__(self, bufs):
        self.bufs = bufs
        self.i = 0

    def get(self):
        b = self.bufs[self.i % len(self.bufs)]
        self.i += 1
        return b


def build_program():
    nc = bass.Bass("TRN2", target_bir_lowering=False)
    S = Sched(nc)

    def din(name, shape):
        return nc.dram_tensor(name, list(shape), F32, kind="ExternalInput").ap()

    xin = din("xin", [HALO + TOK, D])
    yout = nc.dram_tensor("y", [TOK, D], F32, kind="ExternalOutput").ap()
    w_in = {0: din("w_in0", [D, 3 * DI]), 1: din("w_in1", [D, 2 * DI]),
            2: din("w_in2", [D, 4 * DI]), 3: din("w_in3", [D, 3 * DI])}
    w_out = {l: din("w_out%d" % l, [DI, D]) for l in range(4)}
    w_grp = din("w_grp", [4, 512, 512])
    cols_d = din("cols", [128, 129])
    wsT_d = din("wsT", [2, 128, 1024])
    bsbc_d = din("bsbc", [2, 128, 1024])
    mask_d = din("maskT", [128, 128])
    lnbc_d = din("lnbc", [4, 128, 2048])
    pm_d = din("pmats", [128, 20 * 128])
    ident_d = din("ident", [128, 128])

    def sb(name, ncols, dt):
        h = nc.alloc_sbuf_tensor("sb_" + name, [128, ncols], dt)
        return Buf(name, h.ap(), 4 if dt == F32 else 2)

    x_tm = sb("x_tm", 4 * D, F32)
    xT = sb("xT", 8 * TW, BF16)
    GT = sb("GT", 16 * TW, BF16)
    tm = sb("tm", 4 * DI, BF16)
    vcar = sb("vcar", DI, BF16)
    pooled = sb("pooled", 2 * 4 * TW, BF16)
    ring = sb("ring", NSLOT * SLOT, BF16)
    wout = sb("wout", 16 * D, BF16)
    lnbc = sb("lnbc", 2048, F32)
    xbn = sb("xbn", 4 * D, BF16)
    cols = sb("cols", 129, F32)
    cmat = sb("cmat", 2 * 16 * 128, F32)
    wct = sb("wct", 2 * 8 * 128, BF16)
    pm = sb("pm", 20 * 128, BF16)
    ident = sb("ident", 128, BF16)
    ones = sb("ones", 128, BF16)
    qcar = sb("qcar", 32, F32)
    mhalf = sb("mhalf", 1, F32)
    s2k = Rot([sb("s2k%d" % i, 516, F32) for i in range(8)])
    s4k = Rot([sb("s4k%d" % i, 1024, F32) for i in range(3)])
    small = sb("small", 512, F32)
    small_i = [0]

    def sm(n):
        if small_i[0] + n > 512:
            small_i[0] = 0
        a = small_i[0]
        small_i[0] += n
        return small.v(a, a + n)

    banks = []
    for i in range(8):
        h = nc.alloc_psum_tensor("pb%d" % i, [128, 512], F32)
        banks.append(Buf("pb%d" % i, h.ap(), 4))
    bank_rot = Rot(banks)

    def bf16view(buf, ncols):
        return View(buf, buf.ap.bitcast(BF16)[:, 0:ncols], [(0, ncols * 2)])

    for k in ["xld%d" % s for s in range(4)] + ["xst%d" % s for s in range(4)] + \
             ["xb%d" % s for s in range(4)] + \
             ["rg%d" % s for s in range(NSLOT)] + ["wo%d" % j for j in range(4)] + \
             ["ln0", "c0", "c1", "c2", "c3", "c4", "c5"] + ["cv%d" % j for j in range(NCONV)]:
        S.newsem(k)

    def dma(eng, sem, out_v, in_ap, reads=(), out_ap=None, n=1, fn=None):
        oap = out_v.ap if out_ap is None else out_ap
        if fn is None:
            def fn(e, oap=oap, in_ap=in_ap):
                return e.dma_start(out=oap, in_=in_ap)
        return S.op(eng, fn, reads=reads, writes=[out_v], dma=sem, ndma=n)

    dma("sp", "c0", cols.all(), cols_d)
    dma("pool", "c1", ident.all(), ident_d)
    dma("pool", "c2", pm.all(), pm_d)
    S.op("dve", lambda e: e.memset(ones.ap, 1.0), writes=[ones.all()])
    S.op("pool", lambda e: e.memset(vcar.ap, 0.0), writes=[vcar.all()])
    S.op("pool", lambda e: e.memset(qcar.ap, 0.0), writes=[qcar.all()])
    S.op("pool", lambda e: e.memset(mhalf.ap, -0.5), writes=[mhalf.all()])

    def col(i):
        return cols.v(i, i + 1)

    mk = s2k.get()
    mkv = mk.v(0, 128)
    dma("sp", "c3", mkv, mask_d)
    for a in range(2):
        wsb = s4k.get()
        bsb = s4k.get()
        dma("sp", "c4", wsb.all(), wsT_d[a])
        dma("sp", "c5", bsb.all(), bsbc_d[a])
        wv = wct.v(a * 1024, (a + 1) * 1024)
        S.op("dve",
             lambda e, o=wv.ap.rearrange("p (g t) -> p g t", g=8),
             i0=wsb.ap.rearrange("p (g t) -> p g t", g=8),
             i1=mkv.ap.unsqueeze(1).broadcast_to([128, 8, 128]):
             e.tensor_tensor(out=o, in0=i0, in1=i1, op=ALU.mult),
             reads=[wsb.all(), mkv], writes=[wv])
        rb = [bank_rot.get(), bank_rot.get()]
        for hh in range(2):
            def f(e, hh=hh, rb=rb, a=a):
                r = None
                for gg in range(4):
                    g = hh * 4 + gg
                    r = e.matmul(rb[hh].ap[:, gg * 128:(gg + 1) * 128], ones.ap,
                                 wct.ap[:, a * 1024 + g * 128: a * 1024 + (g + 1) * 128],
                                 start=True, stop=True)
                return r
            PE_LABELS.extend(["const"] * 4)
            S.op("pe", f, reads=[ones.all(), wv], writes=[rb[hh].all()])
        for c in range(16):
            g = c // 2
            rbv = rb[g // 4].v((g % 4) * 128, (g % 4 + 1) * 128)
            cv = cmat.v((a * 16 + c) * 128, (a * 16 + c + 1) * 128)
            vb = col(16 + 32 * a + c)
            bv = bsb.v(g * 128, (g + 1) * 128)
            S.op("dve", lambda e, o=cv.ap, i0=rbv.ap, sc=vb.ap, i1=bv.ap:
                 e.scalar_tensor_tensor(out=o, in0=i0, scalar=sc, in1=i1, op0=ALU.mult, op1=ALU.add),
                 reads=[rbv, vb, bv], writes=[cv])

    def layer_stages(l):
        if l in (0, 3):
            return [(l, k, j) for k in ("v", "u", "z") for j in range(4)]
        if l == 1:
            st = [(l, "v", j) for j in range(4)]
            for g in range(4):
                st += [(l, "grp", g), (l, "z", g)]
            return st
        return [(l, "c4", c) for c in range(16)]

    canon = []
    for l in range(4):
        canon += [(l, "wo", j) for j in range(4)]
        canon += layer_stages(l)
    sidx = {d: i for i, d in enumerate(canon)}
    NST = len(canon)
    wsc_t = nc.dram_tensor("wsc", [NST, 128, SLOT], BF16, kind="Internal")
    wsc_ap = wsc_t.ap()

    class DBuf(Buf):
        def __init__(self, name, nbytes):
            self.name = name
            self.nbytes = nbytes
            self.segs = [[0, nbytes, None, {}]]

    wsc = DBuf("wsc", NST * 128 * SLOT * 2)
    cvdummy = [DBuf("cvd%d" % j, 4) for j in range(NCONV)]

    def wsc_v(i):
        return View(wsc, wsc_ap[i], [(i * 128 * SLOT * 2, (i + 1) * 128 * SLOT * 2)])

    def stage_pairs(desc, dst):
        l, kind, idx = desc
        if kind == "wo":
            return [(dst.rearrange("p (c m) -> p c m", c=4),
                     w_out[l][idx * 512:(idx + 1) * 512, :].rearrange("(c p) m -> p c m", p=128))]
        w = w_in[l]
        if kind in ("v", "u", "z"):
            if l in (0, 3):
                base = {"u": 0, "v": DI, "z": 2 * DI}[kind]
            else:
                base = {"v": 0, "z": DI}[kind]
            c0 = base + idx * 512
            return [(dst.rearrange("p (d f) -> p d f", d=8),
                     w[:, c0:c0 + 512].rearrange("(d p) f -> p d f", p=128))]
        if kind == "grp":
            return [(dst[:, 0:2048].rearrange("p (d e) -> p d e", d=4),
                     w_grp[idx].rearrange("(d p) e -> p d e", p=128))]
        if kind == "c4":
            d4 = dst.rearrange("p (d b f) -> p d b f", d=8, b=4)
            return [(d4[:, :, b, :],
                     w[:, b * DI + idx * 128: b * DI + (idx + 1) * 128].rearrange("(d p) f -> p d f", p=128))
                    for b in range(4)]
        raise ValueError(kind)

    conv_state = {"n": 0}

    def ensure_conv(upto):
        while conv_state["n"] < min(NST, upto):
            i = conv_state["n"]
            pairs = stage_pairs(canon[i], wsc_ap[i])
            j = i % NCONV

            def fn(e, pairs=pairs):
                return [e.dma_start(out=o, in_=i_) for o, i_ in pairs]
            S.op("pool", fn, writes=[wsc_v(i), View(cvdummy[j], None, [(0, 4)])], dma="cv%d" % j, ndma=len(pairs))
            conv_state["n"] = i + 1

    plan = []
    ring_state = {"issued": 0, "consumed": 0}

    def ring_issue():
        i = ring_state["issued"]
        if i >= len(plan):
            return
        slot = i % NSLOT
        sv = ring.v(slot * SLOT, (slot + 1) * SLOT)
        ensure_conv(sidx[plan[i]] + NCONV)
        src = wsc_v(sidx[plan[i]])
        dma("sp", "rg%d" % slot, sv, src.ap, reads=[src])
        ring_state["issued"] = i + 1

    def ring_next(desc):
        i = ring_state["consumed"]
        assert plan[i] == desc, (plan[i], desc)
        assert i < ring_state["issued"]
        slot = i % NSLOT
        return slot * SLOT

    def ring_next_k(desc, k):
        i = ring_state["consumed"] + k
        assert plan[i] == desc, (plan[i], desc)
        assert i < ring_state["issued"]
        return (i % NSLOT) * SLOT

    def ring_next2(desc):
        i = ring_state["consumed"] + 1
        assert plan[i] == desc, (plan[i], desc)
        assert i < ring_state["issued"]
        return (i % NSLOT) * SLOT

    def ring_release():
        ring_state["consumed"] += 1
        ring_issue()

    tiles = [(TW * i, 4) for i in range(6)] + [(3072 + 384 * i, 3) for i in range(3)]
    assert tiles[-1][0] + 128 * tiles[-1][1] == HALO + TOK
    for tok0, NS in tiles:
        for l in range(DBG_LAYERS):
            plan.extend(layer_stages(l))

    def bank():
        return bank_rot.get()

    cur_label = ["init"]

    def mm_group(out_v, pairs, reads, first=True, last=True):
        def f(e, o=out_v.ap, pairs=pairs):
            r = None
            n = len(pairs)
            for i, (l_, r_) in enumerate(pairs):
                r = e.matmul(o, l_, r_, start=(first and i == 0), stop=(last and i == n - 1))
            return r
        PE_LABELS.extend([cur_label[0]] * len(pairs))
        S.op("pe", f, reads=reads, writes=[out_v])

    def xT_v(d, t0, t1):
        return xT.v(d * TW + t0, d * TW + t1)

    def xT_all(t0, t1):
        return View(xT, None, [((d * TW + t0) * 2, (d * TW + t1) * 2) for d in range(8)])

    def xT_build(s, xb):
        tb = bank()
        tv = tb.all()
        tb_bf = tb.ap.bitcast(BF16)

        def f(e, xb=xb, tb_bf=tb_bf):
            r = None
            for c in range(8):
                r = e.transpose(tb_bf[:, c * 128:(c + 1) * 128], xb.ap[:, c * 128:(c + 1) * 128], ident.ap)
            return r
        PE_LABELS.extend(["T"] * 8)
        S.op("pe", f, reads=[xb, ident.all()], writes=[tv])
        ov = xT_all(s * 128, (s + 1) * 128)
        oap = xT.ap.rearrange("p (d t) -> p d t", d=8)[:, :, s * 128:(s + 1) * 128]
        iap = tb_bf.rearrange("p (d t) -> p d t", d=8)
        S.op("act", lambda e, o=oap, i=iap: e.activation(out=o, in_=i, func=AF.Copy),
             reads=[tv], writes=[ov])

    def ln_small(stats_v, nst):
        mv = sm(2)
        S.op("dve", lambda e, o=mv.ap, i=stats_v.ap: e.bn_aggr(out=o, in_=i), reads=[stats_v], writes=[mv])
        ve = sm(1)
        S.op("dve", lambda e, o=ve.ap, i=mv.ap[:, 1:2]:
             e.tensor_scalar(out=o, in0=i, scalar1=EPS, scalar2=None, op0=ALU.add),
             reads=[mv], writes=[ve])
        rstd = sm(1)
        S.op("pool", lambda e, o=rstd.ap, i=ve.ap, h=mhalf.ap: e.tensor_tensor(out=o, in0=i, in1=h, op=ALU.pow),
             reads=[ve, mhalf.all()], writes=[rstd])
        nmr = sm(1)
        S.op("dve", lambda e, o=nmr.ap, i=mv.ap[:, 0:1], r=rstd.ap:
             e.scalar_tensor_tensor(out=o, in0=i, scalar=-1.0, in1=r, op0=ALU.mult, op1=ALU.mult),
             reads=[mv, rstd], writes=[nmr])
        return rstd, nmr

    first_ln = [True]

    def start_layer_loads(l):
        ensure_conv(sidx[(l, "wo", 3)] + NCONV)
        for j in range(4):
            ov = wout.v(j * 4096, (j + 1) * 4096)
            src = wsc_v(sidx[(l, "wo", j)])
            dma("sp", "wo%d" % j, ov, src.ap, reads=[src])
        if first_ln[0]:
            first_ln[0] = False
        else:
            dma("pool", "ln0", lnbc.all(), lnbc_d[l])

    def wout_and_epilogue(l, NS, tok0, final, next_ns=0, skip0=False):
        cur_label[0] = "L%d:wout" % l
        gv = lnbc.v(0, 1024)
        bv = lnbc.v(1024, 2048)
        xbs = []
        for s in range(NS):
            if final and skip0 and s == 0:
                if s < next_ns:
                    xT_build(s, xbn.v(s * D, (s + 1) * D))
                continue
            bs_ = [bank(), bank()]
            for h in range(2):
                for (c0, c1) in ((0, 12), (12, 16)):
                    pairs = [(GT.ap[:, c * TW + s * 128: c * TW + (s + 1) * 128],
                              wout.ap[:, c * 1024 + h * 512: c * 1024 + (h + 1) * 512]) for c in range(c0, c1)]
                    rd = [GT.v(c * TW + s * 128, c * TW + (s + 1) * 128) for c in range(c0, c1)] + \
                         [wout.v(c0 * 1024, c1 * 1024)]
                    mm_group(bs_[h].all(), pairs, rd, first=(c0 == 0), last=(c1 == 16))
            if final and s < next_ns:
                xT_build(s, xbn.v(s * D, (s + 1) * D))
            sbuf_ = s4k.get()
            xs = x_tm.v(s * D, (s + 1) * D)
            st = sm(12)
            for h in range(2):
                sv = sbuf_.v(h * 512, (h + 1) * 512)
                xh = x_tm.v(s * D + h * 512, s * D + (h + 1) * 512)
                S.op("dve", lambda e, o=sv.ap, i0=xh.ap, i1=bs_[h].ap:
                     e.scalar_tensor_tensor(out=o, in0=i0, scalar=ALPHA, in1=i1, op0=ALU.mult, op1=ALU.add),
                     reads=[xh, bs_[h].all()], writes=[sv])
                stv = View(small, st.ap[:, h * 6:(h + 1) * 6], [(st.ivs[0][0] + h * 24, st.ivs[0][0] + (h + 1) * 24)])
                S.op("dve", lambda e, o=stv.ap, i=sv.ap: e.bn_stats(out=o, in_=i), reads=[sv], writes=[stv])
            rstd, nmr = ln_small(st, 2)
            sa = sbuf_.all()
            S.op("act", lambda e, o=sa.ap, r=rstd.ap, n=nmr.ap:
                 e.activation(out=o, in_=o, func=AF.Identity, bias=n, scale=r),
                 reads=[sa, rstd, nmr], writes=[sa])
            S.op("dve", lambda e, o=sa.ap, g=gv.ap: e.tensor_tensor(out=o, in0=o, in1=g, op=ALU.mult),
                 reads=[sa, gv], writes=[sa])
            if final:
                yo = s4k.get().all()
                S.op("dve", lambda e, o=yo.ap, i=sa.ap, b=bv.ap: e.tensor_tensor(out=o, in0=i, in1=b, op=ALU.add),
                     reads=[sa, bv], writes=[yo])
                r0 = tok0 - HALO + s * 128
                assert 0 <= r0 <= TOK - 128
                S.op("pool", lambda e, o=yout[r0:r0 + 128, :], i=yo.ap: e.dma_start(out=o, in_=i),
                     reads=[yo], dma="xst%d" % s)
            else:
                xb = bf16view(s2k.get(), 1024)
                S.op("dve", lambda e, o=xb.ap, i=sa.ap, b=bv.ap: e.tensor_tensor(out=o, in0=i, in1=b, op=ALU.add),
                     reads=[sa, bv], writes=[xb])
                S.op("pool", lambda e, o=xs.ap, i=sa.ap, b=bv.ap: e.tensor_tensor(out=o, in0=i, in1=b, op=ALU.add),
                     reads=[sa, bv], writes=[xs])
                xbs.append((s, xb))
        if final:
            return None
        for s, xb in xbs[:-1]:
            xT_build(s, xb)
        return xbs[-1]

    def v_part(l, NS, func, with_stats, pend=None):
        cur_label[0] = "L%d:v" % l
        stv = [sm(24) for _ in range(NS)] if with_stats else None
        sos = {}

        def one(j, s):
            so = sos[j]
            b = bank()
            pairs = [(xT.ap[:, d * TW + s * 128: d * TW + (s + 1) * 128],
                      ring.ap[:, so + d * 512: so + (d + 1) * 512]) for d in range(8)]
            mm_group(b.all(), pairs, [xT_all(s * 128, (s + 1) * 128), ring.v(so, so + SLOT)])
            tv = tm.v(s * DI + j * 512, s * DI + (j + 1) * 512)
            S.op("act", lambda e, o=tv.ap, i=b.ap, func=func: e.activation(out=o, in_=i, func=func),
                 reads=[b.all()], writes=[tv])
            if with_stats:
                lo = stv[s].ivs[0][0]
                sv = View(small, stv[s].ap[:, j * 6:(j + 1) * 6], [(lo + j * 24, lo + (j + 1) * 24)])
                S.op("dve", lambda e, o=sv.ap, i=tv.ap: e.bn_stats(out=o, in_=i), reads=[tv], writes=[sv])

        if pend is not None and NS > 1:
            assert NSLOT >= 4
            for j in range(4):
                sos[j] = ring_next_k((l, "v", j), j)
            for s in range(NS):
                if s == pend[0]:
                    xT_build(*pend)
                for j in range(4):
                    one(j, s)
            for j in range(4):
                ring_release()
        else:
            if pend is not None:
                xT_build(*pend)
            for j in range(4):
                sos[j] = ring_next((l, "v", j))
                for s in range(NS):
                    one(j, s)
                ring_release()
        return stv

    def fm_group(so, k, W, stride=512, boff=0):
        b = bank()
        pairs = [(ring.ap[:, so + d * stride + boff + k * 128: so + d * stride + boff + (k + 1) * 128],
                  xT.ap[:, d * TW: d * TW + W]) for d in range(8)]
        mm_group(b.v(0, W), pairs, [xT_all(0, W), ring.v(so, so + SLOT)])
        return b

    def layer_A(l, a, NS, W, tok0, final, next_ns=0, pend=None, skip0=False):
        start_layer_loads(l)
        stv = v_part(l, NS, AF.Gelu_apprx_tanh, True, pend)
        cur_label[0] = "L%d:uz" % l
        for s in range(NS):
            rstd, nmr = ln_small(stv[s], 4)
            tv = tm.v(s * DI, (s + 1) * DI)
            S.op("dve", lambda e, o=tv.ap, r=rstd.ap, n=nmr.ap:
                 e.tensor_scalar(out=o, in0=o, scalar1=r, scalar2=n, op0=ALU.mult, op1=ALU.add),
                 reads=[tv, rstd, nmr], writes=[tv])
        for j in range(4):
            so = ring_next((l, "u", j))
            for k in range(4):
                c = 4 * j + k
                b = fm_group(so, k, W)
                gv_ = GT.v(c * TW, c * TW + W)
                S.op("act", lambda e, o=gv_.ap, i=b.ap[:, 0:W]: e.activation(out=o, in_=i, func=AF.Gelu_apprx_tanh),
                     reads=[b.v(0, W)], writes=[gv_])
            ring_release()
        for j in range(4):
            so = ring_next((l, "z", j))
            for k in range(4):
                c = 4 * j + k
                g = c // 2
                bz = fm_group(so, k, W)
                szb = s2k.get()
                sz = bf16view(szb, W)
                S.op("act", lambda e, o=sz.ap, i=bz.ap[:, 0:W]: e.activation(out=o, in_=i, func=AF.Silu),
                     reads=[bz.v(0, W)], writes=[sz])
                bs_ = bank()
                wv = wct.v(a * 1024 + g * 128, a * 1024 + (g + 1) * 128)

                def f(e, bs_=bs_, c=c, wv=wv):
                    r = None
                    for s in range(NS):
                        r = e.matmul(bs_.ap[:, s * 128:(s + 1) * 128],
                                     tm.ap[:, s * DI + c * 128: s * DI + (c + 1) * 128], wv.ap,
                                     start=True, stop=True)
                    return r
                PE_LABELS.extend([cur_label[0] + ":spat"] * NS)
                S.op("pe", f, reads=[tm.v(s * DI + c * 128, s * DI + (c + 1) * 128) for s in range(NS)] + [wv],
                     writes=[bs_.v(0, W)])
                tb = s2k.get()
                tmp = tb.v(0, W)
                cv = cmat.v((a * 16 + c) * 128, (a * 16 + c + 1) * 128)
                gc = col(32 * a + c)
                S.op("dve", lambda e, o=tmp.ap.rearrange("p (s t) -> p s t", s=NS),
                     i0=bs_.ap[:, 0:W].rearrange("p (s t) -> p s t", s=NS), sc=gc.ap,
                     i1=cv.ap.unsqueeze(1).broadcast_to([128, NS, 128]):
                     e.scalar_tensor_tensor(out=o, in0=i0, scalar=sc, in1=i1, op0=ALU.mult, op1=ALU.add),
                     reads=[bs_.v(0, W), gc, cv], writes=[tmp])
                S.op("dve", lambda e, o=tmp.ap, i1=sz.ap: e.tensor_tensor(out=o, in0=o, in1=i1, op=ALU.mult),
                     reads=[tmp, sz], writes=[tmp])
                gv_ = GT.v(c * TW, c * TW + W)
                S.op("dve", lambda e, o=gv_.ap, i1=tmp.ap: e.tensor_tensor(out=o, in0=o, in1=i1, op=ALU.mult),
                     reads=[gv_, tmp], writes=[gv_])
            ring_release()
        return wout_and_epilogue(l, NS, tok0, final, next_ns, skip0)

    def layer_B(l, NS, W, tok0, special_s, final=False, next_ns=0, pend=None, skip0=False):
        start_layer_loads(l)
        v_part(l, NS, AF.Copy, False, pend)
        cur_label[0] = "L%d:pz" % l
        for g in range(4):
            so_g = ring_next((l, "grp", g))
            pset = g % 2
            for k in range(4):
                c = 4 * g + k
                bp = bank()

                def f(e, bp=bp, c=c, g=g):
                    r = None
                    for s in range(NS):
                        cur = tm.ap[:, s * DI + c * 128: s * DI + (c + 1) * 128]
                        prv = (tm.ap[:, (s - 1) * DI + c * 128:(s - 1) * DI + (c + 1) * 128] if s > 0
                               else vcar.ap[:, c * 128:(c + 1) * 128])
                        if special_s is not None and s == special_s:
                            ms = [(cur, 8 + g), (cur, 12 + g), (prv, 16 + g)]
                        else:
                            ms = [(cur, g), (prv, 4 + g)]
                        for i, (lt, mi) in enumerate(ms):
                            r = e.matmul(bp.ap[:, s * 128:(s + 1) * 128], lt, pm.ap[:, mi * 128:(mi + 1) * 128],
                                         start=(i == 0), stop=(i == len(ms) - 1))
                    return r
                PE_LABELS.extend(["L1:pool"] * (2 * NS + (1 if special_s is not None else 0)))
                S.op("pe", f,
                     reads=[tm.v(s * DI + c * 128, s * DI + (c + 1) * 128) for s in range(NS)] +
                           [vcar.v(c * 128, (c + 1) * 128), pm.all()],
                     writes=[bp.v(0, W)])
                pv = pooled.v((pset * 4 + k) * TW, (pset * 4 + k) * TW + W)
                S.op("dve", lambda e, o=pv.ap, i=bp.ap[:, 0:W]: e.tensor_copy(out=o, in_=i),
                     reads=[bp.v(0, W)], writes=[pv])
            so_z = ring_next2((l, "z", g))
            for k2 in range(4):
                ec = 4 * g + k2
                bm = bank()
                pairs = [(ring.ap[:, so_g + dk * 512 + k2 * 128: so_g + dk * 512 + (k2 + 1) * 128],
                          pooled.ap[:, (pset * 4 + dk) * TW:(pset * 4 + dk) * TW + W]) for dk in range(4)]
                mm_group(bm.v(0, W), pairs,
                         [ring.v(so_g, so_g + 2048)] +
                         [pooled.v((pset * 4 + dk) * TW, (pset * 4 + dk) * TW + W) for dk in range(4)])
                bz = fm_group(so_z, k2, W)
                szb = s2k.get()
                sz = bf16view(szb, W)
                S.op("act", lambda e, o=sz.ap, i=bz.ap[:, 0:W]: e.activation(out=o, in_=i, func=AF.Silu),
                     reads=[bz.v(0, W)], writes=[sz])
                gv_ = GT.v(ec * TW, ec * TW + W)
                sc = col(64 + ec)
                S.op("dve", lambda e, o=gv_.ap, i0=bm.ap[:, 0:W], sc=sc.ap, i1=sz.ap:
                     e.scalar_tensor_tensor(out=o, in0=i0, scalar=sc, in1=i1, op0=ALU.mult, op1=ALU.mult),
                     reads=[bm.v(0, W), sc, sz], writes=[gv_])
            ring_release()
            ring_release()
        lastv = tm.v((NS - 1) * DI, NS * DI)
        S.op("pool", lambda e, o=vcar.ap, i=lastv.ap: e.tensor_copy(out=o, in_=i),
             reads=[lastv], writes=[vcar.all()])
        return wout_and_epilogue(l, NS, tok0, final, next_ns, skip0)

    def layer_C(l, NS, W, tok0, halo0, final=False, next_ns=0, pend=None, skip0=False):
        start_layer_loads(l)
        if pend is not None:
            xT_build(*pend)
        cur_label[0] = "L%d:c" % l
        for c in range(16):
            so = ring_next((l, "c4", c))
            bb = fm_group(so, 0, W, boff=0)
            bc = fm_group(so, 0, W, boff=128)
            bh = fm_group(so, 0, W, boff=256)
            bz = fm_group(so, 0, W, boff=384)
            ring_release()
            csb = s2k.get()
            cs = csb.v(0, W)
            S.op("act", lambda e, o=cs.ap, i=bc.ap[:, 0:W]: e.activation(out=o, in_=i, func=AF.Copy),
                 reads=[bc.v(0, W)], writes=[cs])
            qb = s2k.get()
            qc = qcar.v(2 * c, 2 * c + 2)
            q0 = qb.v(0, 2)
            S.op("pool", lambda e, o=q0.ap, i=qc.ap: e.tensor_copy(out=o, in_=i), reads=[qc], writes=[q0])
            qm = qb.v(2, 2 + W)
            S.op("dve", lambda e, o=qm.ap, i0=bh.ap[:, 0:W], i1=cs.ap: e.tensor_tensor(out=o, in0=i0, in1=i1, op=ALU.mult),
                 reads=[bh.v(0, W), cs], writes=[qm])
            if halo0:
                qh = qb.v(2, 2 + 128)
                fl = col(128)
                S.op("dve", lambda e, o=qh.ap, f_=fl.ap:
                     e.tensor_scalar(out=o, in0=o, scalar1=f_, scalar2=None, op0=ALU.mult),
                     reads=[qh, fl], writes=[qh])
            qt = qb.v(W, W + 2)
            S.op("pool", lambda e, o=qc.ap, i=qt.ap: e.tensor_copy(out=o, in_=i), reads=[qt], writes=[qc])
            szb = s2k.get()
            sz = bf16view(szb, W)
            S.op("act", lambda e, o=sz.ap, i=bz.ap[:, 0:W]: e.activation(out=o, in_=i, func=AF.Silu),
                 reads=[bz.v(0, W)], writes=[sz])
            tb = s2k.get()
            t = tb.v(0, W)
            w0, w1, w2 = col(80 + c), col(96 + c), col(112 + c)
            S.op("dve", lambda e, o=t.ap, i=qm.ap, sc=w2.ap:
                 e.tensor_scalar(out=o, in0=i, scalar1=sc, scalar2=None, op0=ALU.mult),
                 reads=[qm, w2], writes=[t])
            q1 = qb.v(1, 1 + W)
            S.op("dve", lambda e, o=t.ap, i=q1.ap, sc=w1.ap:
                 e.scalar_tensor_tensor(out=o, in0=i, scalar=sc, in1=o, op0=ALU.mult, op1=ALU.add),
                 reads=[q1, w1, t], writes=[t])
            q2 = qb.v(0, W)
            S.op("dve", lambda e, o=t.ap, i=q2.ap, sc=w0.ap:
                 e.scalar_tensor_tensor(out=o, in0=i, scalar=sc, in1=o, op0=ALU.mult, op1=ALU.add),
                 reads=[q2, w0, t], writes=[t])
            S.op("dve", lambda e, o=t.ap, i1=bb.ap[:, 0:W]: e.tensor_tensor(out=o, in0=o, in1=i1, op=ALU.mult),
                 reads=[t, bb.v(0, W)], writes=[t])
            gv_ = GT.v(c * TW, c * TW + W)
            S.op("dve", lambda e, o=gv_.ap, i0=t.ap, i1=sz.ap: e.tensor_tensor(out=o, in0=i0, in1=i1, op=ALU.mult),
                 reads=[t, sz], writes=[gv_])
        return wout_and_epilogue(l, NS, tok0, final, next_ns, skip0)

    def load_xbn(tok0, NS):
        for s in range(NS):
            dma("pool", "xb%d" % s, xbn.v(s * D, (s + 1) * D), xin[tok0 + s * 128: tok0 + (s + 1) * 128, :])

    load_xbn(*tiles[0])
    for s in range(tiles[0][1]):
        dma("pool", "xld%d" % s, x_tm.v(s * D, (s + 1) * D), xin[tiles[0][0] + s * 128: tiles[0][0] + (s + 1) * 128, :])
    dma("pool", "ln0", lnbc.all(), lnbc_d[0])
    ensure_conv(NCONV)
    for _ in range(NSLOT):
        ring_issue()
    for s in range(tiles[0][1]):
        xT_build(s, xbn.v(s * D, (s + 1) * D))
    for ti, (tok0, NS) in enumerate(tiles):
        W = NS * 128
        t0 = ti == 0
        nxt = tiles[ti + 1] if ti + 1 < len(tiles) else None
        if not t0:
            for s in range(NS):
                xs = x_tm.v(s * D, (s + 1) * D)
                dma("pool", "xld%d" % s, xs, xin[tok0 + s * 128: tok0 + (s + 1) * 128, :])
        if nxt is not None:
            load_xbn(*nxt)
        next_ns = nxt[1] if nxt is not None else 0
        pend = None
        for l in range(DBG_LAYERS):
            final = l == DBG_LAYERS - 1
            nn = next_ns if final else 0
            if l in (0, 3):
                pend = layer_A(l, 0 if l == 0 else 1, NS, W, tok0, final, nn, pend, t0)
            elif l == 1:
                pend = layer_B(l, NS, W, tok0, 1 if t0 else None, final, nn, pend, t0)
            else:
                pend = layer_C(l, NS, W, tok0, t0, final, nn, pend, t0)
    assert ring_state["consumed"] == len(plan)
    S.final_wait("sp", ["xst%d" % s for s in range(4)])
    S.emit()
    return nc


def _colvec(v):
    return np.ascontiguousarray(np.asarray(v, np.float32).reshape(16, 128).T)


def _pool_mats(second_half):
    m = np.zeros((20, 128, 128), np.float32)
    s = np.arange(128)[:, None]
    t = np.arange(128)[None, :]
    eye = (s == t).astype(np.float32)
    for g, w in enumerate(POOL_W):
        band = ((t - s >= 0) & (t - s <= w - 1)).astype(np.float32)
        main = band / w - eye
        prev = ((t + 128 - s) <= w - 1).astype(np.float32) / w
        m[g] = main
        m[4 + g] = prev
        if second_half:
            m[8 + g] = main
            m[16 + g] = prev
        else:
            cnt = np.minimum(np.arange(128) + 1, w).astype(np.float32)[None, :]
            start = band / cnt - eye
            import ml_dtypes
            hi = start.astype(ml_dtypes.bfloat16).astype(np.float32)
            m[8 + g] = hi
            m[12 + g] = start - hi
    return np.ascontiguousarray(m.transpose(1, 0, 2).reshape(128, 20 * 128))


_NC_CACHE = {}


def kernel(**inputs):
    f = lambda k: np.asarray(inputs[k], np.float32)
    x = f("x")
    B, SEQ, _ = x.shape
    assert (B, SEQ) == (4, 8192)
    cols_base = np.zeros((128, 129), np.float32)
    cols_base[:, 0:16] = _colvec(f("a0_v_gain"))
    cols_base[:, 16:32] = _colvec(f("a0_v_bias"))
    cols_base[:, 32:48] = _colvec(f("a3_v_gain"))
    cols_base[:, 48:64] = _colvec(f("a3_v_bias"))
    cols_base[:, 64:80] = _colvec(f("b1_scale"))
    cw = f("c2_conv_w")
    for j in range(3):
        cols_base[:, 80 + 16 * j: 96 + 16 * j] = _colvec(cw[j])
    wsT = np.stack([np.ascontiguousarray(f(k).transpose(2, 0, 1)).reshape(128, 1024)
                    for k in ("a0_w_s", "a3_w_s")])
    bsbc = np.stack([np.ascontiguousarray(np.broadcast_to(f(k).reshape(1, 1024), (128, 1024)))
                     for k in ("a0_b_s", "a3_b_s")])
    s_ = np.arange(128)[:, None]
    t_ = np.arange(128)[None, :]
    maskT = (s_ <= t_).astype(np.float32)
    lnbc = np.stack([np.concatenate([np.broadcast_to(f("ln%d_gain" % l)[None, :], (128, 1024)),
                                     np.broadcast_to(f("ln%d_bias" % l)[None, :], (128, 1024))], axis=1)
                     for l in range(4)]).astype(np.float32)
    ident = np.eye(128, dtype=np.float32)
    shared = {
        "w_in0": f("a0_w_in"), "w_in1": f("b1_w_in"), "w_in2": f("c2_w_in"), "w_in3": f("a3_w_in"),
        "w_out0": f("a0_w_out"), "w_out1": f("b1_w_out"), "w_out2": f("c2_w_out"), "w_out3": f("a3_w_out"),
        "w_grp": f("b1_w_grp"), "wsT": wsT, "bsbc": bsbc, "maskT": maskT, "lnbc": np.ascontiguousarray(lnbc),
        "ident": ident,
    }
    in_maps = []
    for i in range(N_CORES):
        b, h = i // 2, i % 2
        main = x[b, h * TOK:(h + 1) * TOK]
        halo = x[b, TOK - HALO:TOK] if h == 1 else x[b, 0:HALO]
        cols = cols_base.copy()
        cols[:, 128] = float(h)
        m = dict(shared)
        m["xin"] = np.ascontiguousarray(np.concatenate([halo, main], axis=0))
        m["cols"] = cols
        m["pmats"] = _pool_mats(h == 1)
        in_maps.append(m)
    if DBG_LAYERS not in _NC_CACHE:
        _NC_CACHE[DBG_LAYERS] = build_program()
    nc = _NC_CACHE[DBG_LAYERS]
    res = run_bass_kernel_spmd(nc, in_maps, core_ids=list(range(N_CORES)))
    out = np.empty((B, SEQ, D), np.float32)
    for i in range(N_CORES):
        b, h = i // 2, i % 2
        out[b, h * TOK:(h + 1) * TOK] = np.asarray(res.results[i]["y"], np.float32)
    return out
```
